# Optimizing a Trainium2 kernel written in Bass

```python
import jax, jax.numpy as jnp
from jax import lax
import numpy as np

D_MODEL = 4096
BATCH = 4
SEQ = 4096
DEPTH = 1

N_Q_HEADS = 64
N_KV_HEADS = 8
HEAD_DIM = 64
Q_PER_KV = N_Q_HEADS // N_KV_HEADS
ATTN_WIDTH = N_Q_HEADS * HEAD_DIM
KV_WIDTH = N_KV_HEADS * HEAD_DIM
WINDOW = 128
BLOCK = 128
ROPE_THETA = 500000.0
ROPE_DIM = HEAD_DIM // 4

GMLP_WIDTH = D_MODEL
GMLP_GROUPS = 8
GMLP_GROUP_DIM = GMLP_WIDTH // GMLP_GROUPS
GMLP_CHUNK = 128

NORM_EPS = 1e-5
LN_EPS = 1e-5

PROJ_SIZES = (ATTN_WIDTH, KV_WIDTH, KV_WIDTH, ATTN_WIDTH,
              GMLP_WIDTH, GMLP_WIDTH, GMLP_WIDTH, D_MODEL, D_MODEL)
PROJ_WIDTH = int(sum(PROJ_SIZES))
SPLIT_POINTS = tuple(int(s) for s in np.cumsum(PROJ_SIZES)[:-1])

kernel_name = "hybrid_swa_sink_gmlp_gated_merge"


def rms_norm(x, g):
    x32 = x.astype(jnp.float32)
    y = x32 * lax.rsqrt(jnp.mean(x32 * x32, axis=-1, keepdims=True) + NORM_EPS)
    return (y * g.astype(jnp.float32)).astype(x.dtype)


def layer_norm(x, g, b):
    x32 = x.astype(jnp.float32)
    mu = jnp.mean(x32, axis=-1, keepdims=True)
    xc = x32 - mu
    y = xc * lax.rsqrt(jnp.mean(xc * xc, axis=-1, keepdims=True) + LN_EPS)
    return (y * g.astype(jnp.float32) + b.astype(jnp.float32)).astype(x.dtype)


def rope_tables(positions, dtype):
    half = ROPE_DIM // 2
    inv_freq = ROPE_THETA ** (-jnp.arange(half, dtype=jnp.float32) * 2.0 / ROPE_DIM)
    ang = positions.astype(jnp.float32)[..., None] * inv_freq
    return jnp.cos(ang)[:, :, None, :].astype(dtype), jnp.sin(ang)[:, :, None, :].astype(dtype)


def partial_rope(t, cos, sin):
    half = ROPE_DIM // 2
    t1 = t[..., :half]
    t2 = t[..., half:ROPE_DIM]
    return jnp.concatenate([t1 * cos - t2 * sin, t2 * cos + t1 * sin, t[..., ROPE_DIM:]], axis=-1)


def sliding_window_sink_attention(q, k, v, sink):
    B, S = q.shape[0], q.shape[1]
    nb = S // BLOCK
    qb = q.reshape(B, nb, BLOCK, N_KV_HEADS, Q_PER_KV, HEAD_DIM)
    kb = k.reshape(B, nb, BLOCK, N_KV_HEADS, HEAD_DIM)
    vb = v.reshape(B, nb, BLOCK, N_KV_HEADS, HEAD_DIM)

    def with_prev(t):
        prev = jnp.concatenate([jnp.zeros_like(t[:, :1]), t[:, :-1]], axis=1)
        return jnp.concatenate([prev, t], axis=2)

    kband, vband = with_prev(kb), with_prev(vb)
    sink_g = sink.astype(jnp.float32).reshape(1, N_KV_HEADS, Q_PER_KV, 1, 1)
    qi = jnp.arange(BLOCK)[:, None]
    si = jnp.arange(2 * BLOCK)[None, :]
    band = (si <= qi + BLOCK) & (si > qi + BLOCK - WINDOW)
    scale = HEAD_DIM ** -0.5

    def one_block(args):
        idx, qx, kx, vx = args
        s = jnp.einsum('bqhgd,bshd->bhgqs', qx, kx,
                       preferred_element_type=jnp.float32) * scale
        mask = band & ((idx > 0) | (si >= BLOCK))
        s = jnp.where(mask, s, -jnp.inf)
        m = jnp.maximum(jnp.max(s, axis=-1, keepdims=True), sink_g)
        p = jnp.exp(s - m)
        denom = jnp.sum(p, axis=-1, keepdims=True) + jnp.exp(sink_g - m)
        return jnp.einsum('bhgqs,bshd->bqhgd', (p / denom).astype(vx.dtype), vx)

    xs = (jnp.arange(nb), jnp.moveaxis(qb, 1, 0), jnp.moveaxis(kband, 1, 0), jnp.moveaxis(vband, 1, 0))
    out = lax.map(one_block, xs)
    return jnp.moveaxis(out, 0, 1).reshape(B, S, ATTN_WIDTH)


def chunked_spatial_gating(u, v, w_s, b_s, ln_g, ln_b):
    B, S, W = v.shape
    nc = S // GMLP_CHUNK
    vn = layer_norm(v, ln_g, ln_b)
    vc = vn.reshape(B, nc, GMLP_CHUNK, GMLP_GROUPS, GMLP_GROUP_DIM)
    causal = jnp.tril(jnp.ones((GMLP_CHUNK, GMLP_CHUNK), dtype=bool))
    w = jnp.where(causal[None], w_s, jnp.zeros_like(w_s)).astype(v.dtype)
    mixed = jnp.einsum('gts,bnsgc->bntgc', w, vc) + b_s.T[:, :, None].astype(v.dtype)
    return u * mixed.reshape(B, S, W)


def setup_inputs(seed: int = 0) -> dict:
    key = jax.random.key(seed)
    ks = jax.random.split(key, 16)
    f32 = jnp.float32
    x = jax.random.normal(ks[0], (BATCH, SEQ, D_MODEL), f32)
    offsets = jax.random.randint(ks[1], (BATCH, 1), 0, 4096, dtype=jnp.int32)
    positions = offsets + jnp.arange(SEQ, dtype=jnp.int32)[None, :]
    norm_g = 1.0 + 0.02 * jax.random.normal(ks[2], (DEPTH, D_MODEL), f32)
    w_in = jax.random.normal(ks[3], (DEPTH, D_MODEL, PROJ_WIDTH), f32) * D_MODEL ** -0.5
    attn_sink = 0.5 * jax.random.normal(ks[4], (DEPTH, N_Q_HEADS), f32)
    gmlp_ln_g = 1.0 + 0.02 * jax.random.normal(ks[5], (DEPTH, GMLP_WIDTH), f32)
    gmlp_ln_b = 0.02 * jax.random.normal(ks[6], (DEPTH, GMLP_WIDTH), f32)
    w_spatial = jax.random.normal(ks[7], (DEPTH, GMLP_GROUPS, GMLP_CHUNK, GMLP_CHUNK), f32) * GMLP_CHUNK ** -0.5
    b_spatial = 1.0 + 0.1 * jax.random.normal(ks[8], (DEPTH, GMLP_GROUPS, GMLP_CHUNK), f32)
    w_up_attn = jax.random.normal(ks[9], (DEPTH, ATTN_WIDTH, D_MODEL), f32) * ATTN_WIDTH ** -0.5
    w_up_gmlp = jax.random.normal(ks[10], (DEPTH, GMLP_WIDTH, D_MODEL), f32) * GMLP_WIDTH ** -0.5
    w_out = jax.random.normal(ks[11], (DEPTH, D_MODEL, D_MODEL), f32) * D_MODEL ** -0.5
    final_norm_g = 1.0 + 0.02 * jax.random.normal(ks[12], (D_MODEL,), f32)
    return {"x": x, "positions": positions, "norm_g": norm_g, "w_in": w_in,
            "attn_sink": attn_sink, "gmlp_ln_g": gmlp_ln_g, "gmlp_ln_b": gmlp_ln_b,
            "w_spatial": w_spatial, "b_spatial": b_spatial, "w_up_attn": w_up_attn,
            "w_up_gmlp": w_up_gmlp, "w_out": w_out, "final_norm_g": final_norm_g}


def reference(x, positions, norm_g, w_in, attn_sink, gmlp_ln_g, gmlp_ln_b, w_spatial,
              b_spatial, w_up_attn, w_up_gmlp, w_out, final_norm_g):
    B, S = x.shape[0], x.shape[1]
    cos, sin = rope_tables(positions, x.dtype)
    for l in range(DEPTH):
        h = rms_norm(x, norm_g[l])
        proj = jnp.einsum('bsd,dp->bsp', h, w_in[l])
        q, k, v, gate_a, u, vg, gate_b, mg_a, mg_b = jnp.split(proj, SPLIT_POINTS, axis=-1)
        q = partial_rope(q.reshape(B, S, N_Q_HEADS, HEAD_DIM), cos, sin)
        k = partial_rope(k.reshape(B, S, N_KV_HEADS, HEAD_DIM), cos, sin)
        v = v.reshape(B, S, N_KV_HEADS, HEAD_DIM)
        attn = sliding_window_sink_attention(q, k, v, attn_sink[l])
        y_a = jnp.einsum('bsw,wd->bsd', attn * jax.nn.silu(gate_a), w_up_attn[l])
        sg = chunked_spatial_gating(jax.nn.gelu(u), jax.nn.gelu(vg), w_spatial[l], b_spatial[l],
                                    gmlp_ln_g[l], gmlp_ln_b[l])
        y_b = jnp.einsum('bsw,wd->bsd', sg * jax.nn.silu(gate_b), w_up_gmlp[l])
        merged = jax.nn.sigmoid(mg_a) * y_a + jax.nn.sigmoid(mg_b) * y_b
        x = x + jnp.einsum('bsd,de->bse', merged, w_out[l])
    return rms_norm(x, final_norm_g)
```

```python
import math
from contextlib import ExitStack

import numpy as np
import concourse.bass as bass
import concourse.mybir as mybir
from concourse.bass_utils import run_bass_kernel_spmd

F32 = mybir.dt.float32
BF16 = mybir.dt.bfloat16
I32 = mybir.dt.int32
AF = mybir.ActivationFunctionType
ALU = mybir.AluOpType
AX = mybir.AxisListType

D = 4096
KC = 32
T = 512
NBK = 4
TOK_PER_CORE = 2048
OFF_Q, OFF_K, OFF_V, OFF_GA, OFF_U, OFF_VG, OFF_GB, OFF_MA, OFF_MB = (
    0, 4096, 4608, 5120, 9216, 13312, 17408, 21504, 25600)
PW = 29696
NEG = -30000.0
PE, ACT, DVE, POOL, SP = "pe", "act", "dve", "pool", "sp"
TWO_PI = 2.0 * math.pi
DEBUG_PARTS = ""
DEBUG_KSTEPS = 99
DEBUG_NTOK = 0
DEBUG_ASTEPS = 99


class Buf:
    def __init__(self, name):
        self.name = name
        self.w = {}
        self.r = {}


class Builder:
    def __init__(self, nc, es, n_epochs):
        self.nc = nc
        self.es = es
        self.q = {e: [] for e in (PE, ACT, DVE, POOL, SP)}
        self.prog = {e: [es.enter_context(nc.semaphore(f"pg_{e}_{i}")) for i in range(n_epochs)]
                     for e in (PE, ACT, DVE)}
        self.cnt = {PE: 0, ACT: 0, DVE: 0}
        self.epoch = 0
        self.grp = 0
        self.n_groups = n_epochs
        self.seen = {}
        self.dsem = {}
        self.dcnt = {}

    def dma_sem(self, name):
        if name not in self.dsem:
            self.dsem[name] = self.es.enter_context(self.nc.semaphore(f"ds_{name}"))
            self.dcnt[name] = 0
        return name

    def _waits(self, eng, deps):
        out = []
        for d in deps:
            if d is None:
                continue
            if d[0] == "sem":
                key = (eng, "sem", d[1])
                if self.seen.get(key, 0) >= d[2]:
                    continue
                self.seen[key] = d[2]
                out.append((self.dsem[d[1]], d[2]))
                continue
            E, ep, n, g = d
            if n == 0:
                continue
            if eng != POOL and ep < self.epoch:
                continue
            if eng == PE and E == PE:
                continue
            key = (eng, E, g)
            if self.seen.get(key, 0) >= n:
                continue
            self.seen[key] = n
            out.append((self.prog[E][g], n))
        return out

    def _deps(self, reads, writes):
        deps = []
        for b in reads:
            deps += list(b.w.values())
        for b in writes:
            deps += list(b.w.values()) + list(b.r.values())
        return deps

    def emit(self, eng, fn, reads=(), writes=(), extra=()):
        deps = self._deps(reads, writes) + list(extra)
        ws = self._waits(eng, deps)
        self.cnt[eng] += 1
        tk = (eng, self.epoch, self.cnt[eng], self.grp)
        sem = self.prog[eng][self.grp]

        def run(e):
            for (sm, v) in ws:
                e.wait_ge(sm, v)
            ins = fn(e)
            ins.then_inc(sem, 1)
        self.q[eng].append(run)
        for b in reads:
            b.r[eng] = tk
        for b in writes:
            b.w[eng] = tk
        return tk

    def dma(self, queue, fns, semname, reads=(), writes=(), extra=()):
        self.dma_sem(semname)
        deps = self._deps(reads, writes) + list(extra)
        if self.dcnt[semname] > 0:
            deps.append(("sem", semname, self.dcnt[semname]))
        ws = self._waits(queue, deps)
        self.dcnt[semname] += 16 * len(fns)
        val = self.dcnt[semname]
        sem = self.dsem[semname]

        def run(e):
            for (sm, v) in ws:
                e.wait_ge(sm, v)
            for f in fns:
                f(e).then_inc(sem, 16)
        self.q[queue].append(run)
        tk = ("sem", semname, val)
        for b in reads:
            b.r["dma_" + semname] = tk
        for b in writes:
            b.w["dma_" + semname] = tk
        return tk

    def barrier(self, new_group=False):
        last = [(E, self.epoch, self.cnt[E], self.grp) for E in (PE, ACT, DVE)]
        for F in (PE, ACT, DVE, SP):
            ws = self._waits(F, [d for d in last if d[0] != F])

            def run(e, ws=ws):
                for (sm, v) in ws:
                    e.wait_ge(sm, v)
            self.q[F].append(run)
        self.epoch += 1
        if new_group:
            self.grp += 1
            assert self.grp < self.n_groups
            for E in self.cnt:
                self.cnt[E] = 0

    def final_wait(self, queue):
        items = [(self.dsem[n], self.dcnt[n]) for n in self.dsem if self.dcnt[n] > 0]

        def run(e):
            for (sm, v) in items:
                e.wait_ge(sm, v)
        self.q[queue].append(run)


def build_nc(n_tiles=4, stop_after=99):
    nc = bass.Bass("TRN2", target_bir_lowering=False)
    ntok_core = n_tiles * T

    def din(name, shape, dt=F32):
        return nc.dram_tensor(name, list(shape), dt, kind="ExternalInput").ap()

    x_d = din("x", [ntok_core, D])
    xh_d = din("xh", [128, D])
    pos_d = din("pos", [1, 128 + ntok_core], I32)
    w_in_d = din("w_in", [D, PW])
    w_ua_d = din("w_ua", [D, D])
    w_ug_d = din("w_ug", [D, D])
    w_out_d = din("w_out", [D, D])
    gcol_d = din("gcol", [128, KC])
    lng_d = din("lng", [128, KC])
    lnb_d = din("lnb", [128, KC])
    fng_d = din("fng", [1, D])
    sink_d = din("sink", [1, 64])
    wsp_d = din("wspT", [128, 8 * 128])
    tril_d = din("tril", [128, 128])
    bsp_d = din("bsp", [1, 8 * 128])
    maskA_d = din("maskA", [128, 256])
    maskB_d = din("maskB", [128, 256])
    invf_d = din("invf", [128, 1])
    ident_d = din("ident", [128, 128])
    perm_d = din("perm", [128, 128])
    swap_d = din("swap", [128, 128])
    y_d = nc.dram_tensor("y", [ntok_core, D], F32, kind="ExternalOutput").ap()

    def bcast_rows(ap_d, ncols, off=0):
        return bass.AP(ap_d.tensor, off, [[0, 128], [1, ncols]])

    with ExitStack() as es:
        B = Builder(nc, es, n_epochs=n_tiles + 1)

        def sb(name, shape, dt):
            return es.enter_context(nc.sbuf_tensor("s_" + name, list(shape), dt))

        def ps(name, shape, dt):
            return es.enter_context(nc.psum_tensor("p_" + name, list(shape), dt))

        hT_t = sb("hT", [128, KC * T], BF16)
        br_t = sb("br", [128, KC * T], BF16)
        mg_t = sb("mg", [128, KC * T], BF16)
        wb_t = [sb(f"wb{i}", [128, KC * 256], BF16) for i in range(2)]
        r1_t = sb("r1", [128, 16384], BF16)
        hT = hT_t[:].rearrange("p (k t) -> p k t", k=KC)
        br = br_t[:].rearrange("p (k t) -> p k t", k=KC)
        mg = mg_t[:].rearrange("p (k t) -> p k t", k=KC)
        wb = [w[:].rearrange("p (k c) -> p k c", k=KC) for w in wb_t]
        ybuf = [hT_t[:, 0:8192].bitcast(F32), hT_t[:, 8192:16384].bitcast(F32),
                br_t[:, 0:8192].bitcast(F32), br_t[:, 8192:16384].bitcast(F32)]
        XA = r1_t[:, 0:8192].bitcast(F32)
        XB = r1_t[:, 8192:16384].bitcast(F32)
        kT = r1_t[:, 0:2560].rearrange("p (c t) -> p c t", c=4)
        kT2 = r1_t[:, 2560:5120].rearrange("p (c t) -> p c t", c=4)
        v_sb = r1_t[:, 5120:7680].rearrange("p (b c) -> p b c", b=5)
        qrot = r1_t[:, 7680:9728].rearrange("p (c t) -> p c t", c=4)
        sgate = r1_t[:, 9728:11776].rearrange("p (c t) -> p c t", c=4)
        scm = r1_t[:, 11776:13824].bitcast(F32).rearrange("p (j s) -> p j s", j=4)
        Pm = r1_t[:, 13824:14848].rearrange("p (j s) -> p j s", j=4)
        PTs = r1_t[:, 14848:15872]
        gvb = r1_t[:].rearrange("p (b c) -> p b c", b=4)

        cosT = sb("cosT", [128, T], F32)
        sinT = sb("sinT", [128, T], F32)
        posb = sb("posb", [128, T], I32)
        kprev = sb("kprev", [128, 4, 128], BF16)
        k2prev = sb("k2prev", [128, 4, 128], BF16)
        vprev = sb("vprev", [128, 512], BF16)
        mask4 = [sb("maskA4", [128, 4, 256], BF16), sb("maskB4", [128, 4, 256], BF16)]
        identf = sb("identf", [128, 128], F32)
        onesf = sb("onesf", [128, 128], F32)
        identb = sb("identb", [128, 128], BF16)
        permb = sb("permb", [128, 128], BF16)
        swapb = sb("swapb", [128, 128], BF16)
        trilf = sb("trilf", [128, 128], F32)
        WsT = sb("WsT", [128, 8, 128], BF16)
        RSb = sb("RSb", [128, 8, 128], F32)
        BSb = sb("BSb", [128, 8, 128], F32)
        gcol = sb("gcol", [128, KC], F32)
        lngc = sb("lngc", [128, KC], F32)
        lnbc = sb("lnbc", [128, KC], F32)
        sinkB = sb("sinkB", [128, 64], F32)
        negsink = sb("negsink", [128, 64], F32)
        invf = sb("invf", [128, 1], F32)
        epsc = sb("epsc", [128, 1], F32)
        halfpi = sb("halfpi", [128, 1], F32)
        sig = sb("sig", [128, 2, T], BF16)
        ub = sb("ub", [128, 2, T], BF16)
        gb = sb("gb", [128, 2, T], BF16)
        t1 = sb("t1", [128, T], F32)
        t2 = sb("t2", [128, T], F32)
        qTf = sb("qTf", [128, T], BF16)
        a_tok = sb("a_tok", [128, 512], BF16)
        tmpb = sb("tmpb", [128, 4, 128], F32)
        mixed4 = sb("mixed4", [128, 4, 128], F32)
        st = sb("st", [128, 8, 6], F32)
        mv = sb("mv", [128, 2], F32)
        ms = sb("ms", [128, 1], F32)
        rstd = sb("rstd", [128, 1], F32)
        sm_rowmax = sb("rowmax", [128, 4], F32)
        sm_negm = sb("negm", [128, 4], F32)
        sm_tmp4 = sb("tmp4", [128, 4], F32)
        sm_rowsum = sb("rowsum", [128, 4], F32)
        sm_es = sb("es", [128, 4], F32)
        sm_den = sb("den", [128, 4], F32)
        rden = sb("rden", [128, 8], F32)

        pj = [ps("pj0", [128, 512], F32), ps("pj1", [128, 512], F32)]
        sc = ps("sc", [128, 1024], F32)
        pt = ps("pt", [128, 1024], BF16)
        pv = ps("pv", [128, 512], F32)
        tr = ps("tr", [128, 1024], BF16)
        rp = ps("rp", [128, 512], F32)
        sc3 = sc[:].rearrange("p (j s) -> p j s", j=4)

        b_hT, b_br, b_mg = Buf("hT"), Buf("br"), Buf("mg")
        b_wb = [Buf("wb0"), Buf("wb1")]
        b_r1a, b_r1b = Buf("r1a"), Buf("r1b")
        b_pj = [Buf("pj0"), Buf("pj1")]
        b_sc = [Buf("sc0"), Buf("sc1")]
        b_scall, b_pt, b_pv, b_tr, b_rp = Buf("sc"), Buf("pt"), Buf("pv"), Buf("tr"), Buf("rp")
        b_cs, b_posb = Buf("cossin"), Buf("posb")
        b_kT, b_kT2, b_v, b_qrot, b_sgate, b_scm, b_P, b_PTs = (Buf(n) for n in
            ("kT", "kT2", "v", "qrot", "sgate", "scm", "P", "PTs"))
        b_kprev, b_const = Buf("kprev"), Buf("const")
        b_sig, b_ub, b_gb, b_t1, b_t2, b_qTf, b_atok = (Buf(n) for n in
            ("sig", "ub", "gb", "t1", "t2", "qTf", "atok"))
        b_tmpb, b_mixed, b_st, b_small, b_rden, b_gvb = (Buf(n) for n in
            ("tmpb", "mixed", "st", "small", "rden", "gvb"))
        b_y = [b_hT, b_hT, b_br, b_br]

        state = {"wi": 0, "pj": 0}

        def wload(src_d, col0):
            i = state["wi"]
            state["wi"] += 1
            slot = i % 2
            srcv = src_d.rearrange("(k p) c -> p k c", p=128)
            fns = []
            for j in range(4):
                fns.append(lambda e, j=j, slot=slot: e.dma_start(
                    out=wb[slot][:, j * 8:(j + 1) * 8, :],
                    in_=srcv[:, j * 8:(j + 1) * 8, col0:col0 + 256]))
            B.dma(POOL, fns, f"w{slot}", writes=[b_wb[slot]])
            return slot

        def next_pj():
            p = state["pj"] % 2
            state["pj"] += 1
            return p

        def proj_fm(slot, ch, rhs_fn, b_rhs, ntok):
            p = next_pj()

            def fn(e):
                ins = None
                for kc in range(KC):
                    ins = e.matmul(pj[p][:, 0:ntok], wb[slot][:, kc, ch * 128:(ch + 1) * 128], rhs_fn(kc),
                                   start=(kc == 0), stop=(kc == KC - 1))
                return ins
            B.emit(PE, fn, reads=[b_wb[slot], b_rhs], writes=[b_pj[p]])
            return p

        def proj_tm(slot, lhs_fn, b_lhs):
            p = next_pj()

            def fn(e):
                ins = None
                for kc in range(KC):
                    ins = e.matmul(pj[p][:, 0:256], lhs_fn(kc), wb[slot][:, kc, :],
                                   start=(kc == 0), stop=(kc == KC - 1))
                return ins
            B.emit(PE, fn, reads=[b_wb[slot], b_lhs], writes=[b_pj[p]])
            return p

        def setup():
            loads = [
                (gcol, gcol_d), (lngc, lng_d), (lnbc, lnb_d), (invf, invf_d),
                (identf, ident_d), (trilf, tril_d),
            ]
            fns = [(lambda e, o=o, i=i: e.dma_start(out=o[:], in_=i)) for (o, i) in loads]
            fns.append(lambda e: e.dma_start(out=sinkB[:], in_=bcast_rows(sink_d, 64)))
            fns.append(lambda e: e.dma_start(out=BSb[:].rearrange("p g t -> p (g t)"), in_=bcast_rows(bsp_d, 1024)))
            fns.append(lambda e: e.dma_start(out=XA[:, 0:1024], in_=wsp_d))
            B.dma(SP, fns, "c", writes=[b_const, b_r1a])
            cfns = [
                lambda e: e.dma_start(out=identb[:], in_=ident_d),
                lambda e: e.dma_start(out=permb[:], in_=perm_d),
                lambda e: e.dma_start(out=swapb[:], in_=swap_d),
            ]
            for j in range(4):
                cfns.append(lambda e, j=j: e.dma_start(out=mask4[0][:, j, :], in_=maskA_d))
                cfns.append(lambda e, j=j: e.dma_start(out=mask4[1][:, j, :], in_=maskB_d))
            B.dma(POOL, cfns, "c2", writes=[b_const])
            B.emit(DVE, lambda e: e.memset(onesf[:], 1.0), writes=[b_small])
            B.emit(DVE, lambda e: e.memset(epsc[:], 1e-5), writes=[b_small])
            B.emit(DVE, lambda e: e.memset(halfpi[:], 0.5 * math.pi), writes=[b_small])
            B.emit(DVE, lambda e: e.tensor_scalar(out=negsink[:], in0=sinkB[:], scalar1=-1.0, scalar2=None,
                                                  op0=ALU.mult), reads=[b_const], writes=[b_small])
            wv = XA[:, 0:1024].rearrange("p (g t) -> p g t", g=8)
            B.emit(DVE, lambda e: e.tensor_tensor(out=wv, in0=wv, in1=trilf[:].unsqueeze(1).broadcast_to([128, 8, 128]),
                                                  op=ALU.mult), reads=[b_const, b_r1a], writes=[b_r1a])
            B.emit(DVE, lambda e: e.tensor_copy(out=WsT[:], in_=wv), reads=[b_r1a], writes=[b_const])
            for hh in range(2):
                def fn(e, hh=hh):
                    return e.matmul(pj[hh][:], onesf[:], XA[:, hh * 512:(hh + 1) * 512], start=True, stop=True)
                B.emit(PE, fn, reads=[b_r1a, b_small], writes=[b_pj[hh]])
                B.emit(ACT, lambda e, hh=hh: e.activation(out=RSb[:, hh * 4:(hh + 1) * 4, :].rearrange("p g t -> p (g t)"),
                                                           in_=pj[hh][:], func=AF.Copy),
                       reads=[b_pj[hh]], writes=[b_const])

        def rope_tables(pos_off, ntok):
            B.dma(SP, [lambda e: e.dma_start(out=posb[:, 0:ntok], in_=bcast_rows(pos_d, ntok, pos_off))],
                  "c", writes=[b_posb])
            B.emit(DVE, lambda e: e.tensor_copy(out=t1[:, 0:ntok], in_=posb[:, 0:ntok]),
                   reads=[b_posb], writes=[b_t1])
            B.emit(DVE, lambda e: e.tensor_scalar(out=t1[:, 0:ntok], in0=t1[:, 0:ntok], scalar1=invf[:, 0:1],
                                                  scalar2=None, op0=ALU.mult),
                   reads=[b_t1, b_const], writes=[b_t1])
            n = ntok
            B.emit(DVE, lambda e: e.tensor_scalar(out=t2[:, 0:n], in0=t1[:, 0:n], scalar1=1.0 / TWO_PI,
                                                  scalar2=0.5, op0=ALU.mult, op1=ALU.add),
                   reads=[b_t1], writes=[b_t2])
            B.emit(DVE, lambda e: e.tensor_copy(out=posb[:, 0:n], in_=t2[:, 0:n]), reads=[b_t2], writes=[b_posb])
            B.emit(DVE, lambda e: e.tensor_copy(out=t2[:, 0:n], in_=posb[:, 0:n]), reads=[b_posb], writes=[b_t2])
            B.emit(DVE, lambda e: e.scalar_tensor_tensor(out=t1[:, 0:n], in0=t2[:, 0:n], scalar=-TWO_PI,
                                                         in1=t1[:, 0:n], op0=ALU.mult, op1=ALU.add),
                   reads=[b_t1, b_t2], writes=[b_t1])
            B.emit(ACT, lambda e: e.activation(out=t2[:, 0:n], in_=t1[:, 0:n], func=AF.Sin, scale=0.5),
                   reads=[b_t1], writes=[b_t2])
            B.emit(ACT, lambda e: e.activation(out=cosT[:, 0:n], in_=t1[:, 0:n], func=AF.Sin,
                                               bias=halfpi[:, 0:1], scale=0.5),
                   reads=[b_t1, b_small], writes=[b_cs])
            B.emit(DVE, lambda e: e.scalar_tensor_tensor(out=sinT[:, 0:n], in0=t2[:, 0:n], scalar=2.0,
                                                         in1=cosT[:, 0:n], op0=ALU.mult, op1=ALU.mult),
                   reads=[b_t2, b_cs], writes=[b_cs])
            B.emit(DVE, lambda e: e.tensor_tensor(out=t1[:, 0:n], in0=t2[:, 0:n], in1=t2[:, 0:n], op=ALU.mult),
                   reads=[b_t2], writes=[b_t1])
            B.emit(DVE, lambda e: e.tensor_scalar(out=cosT[:, 0:n], in0=t1[:, 0:n], scalar1=-2.0, scalar2=1.0,
                                                  op0=ALU.mult, op1=ALU.add),
                   reads=[b_t1], writes=[b_cs])

        def row_stats(src, b_src, n_chunks=8, add_sq=True):
            for j in range(n_chunks):
                B.emit(DVE, lambda e, j=j: e.bn_stats(out=st[:, j, :], in_=src[:, j * 512:(j + 1) * 512]),
                       reads=[b_src], writes=[b_st])
            B.emit(DVE, lambda e: e.bn_aggr(out=mv[:], in_=st[:].rearrange("p a b -> p (a b)")),
                   reads=[b_st], writes=[b_small])
            if add_sq:
                B.emit(DVE, lambda e: e.scalar_tensor_tensor(out=ms[:], in0=mv[:, 0:1], scalar=mv[:, 0:1],
                                                             in1=mv[:, 1:2], op0=ALU.mult, op1=ALU.add),
                       reads=[b_small], writes=[b_small])
                src_v = ms[:, 0:1]
            else:
                src_v = mv[:, 1:2]
            B.emit(ACT, lambda e: e.activation(out=rstd[:], in_=src_v, func=AF.Sqrt, bias=epsc[:, 0:1], scale=1.0),
                   reads=[b_small], writes=[b_small])
            B.emit(DVE, lambda e: e.reciprocal(out=rstd[:], in_=rstd[:]),
                   reads=[b_small], writes=[b_small])

        def phase_n(x_src, nblk):
            Xs = [XA, XB]
            bX = [b_r1a, b_r1b]
            for b in range(nblk):
                X = Xs[b % 2]
                bx = bX[b % 2]
                B.dma(SP, [lambda e, b=b, X=X: e.dma_start(out=X[:, 0:2048], in_=x_src[b * 128:(b + 1) * 128, 0:2048]),
                           lambda e, b=b, X=X: e.dma_start(out=X[:, 2048:4096], in_=x_src[b * 128:(b + 1) * 128, 2048:4096])],
                      "xa" if b % 2 == 0 else "xb", writes=[bx])
                row_stats(X, bx)
                B.emit(DVE, lambda e, X=X: e.tensor_scalar(out=X[:], in0=X[:], scalar1=rstd[:, 0:1], scalar2=None,
                                                           op0=ALU.mult),
                       reads=[bx, b_small], writes=[bx])
                for i in range(8):
                    half = i % 2

                    def fn(e, i=i, half=half, X=X):
                        ins = None
                        for j in range(4):
                            kc = 4 * i + j
                            ins = e.transpose(out=sc[:, half * 512 + j * 128: half * 512 + (j + 1) * 128],
                                              in_=X[:, kc * 128:(kc + 1) * 128], identity=identf[:])
                        return ins
                    B.emit(PE, fn, reads=[bx, b_const], writes=[b_sc[half]])
                    for j in range(4):
                        kc = 4 * i + j
                        B.emit(ACT, lambda e, j=j, kc=kc, half=half, b=b: e.activation(
                            out=hT[:, kc, b * 128:(b + 1) * 128],
                            in_=sc[:, half * 512 + j * 128: half * 512 + (j + 1) * 128],
                            func=AF.Copy, scale=gcol[:, kc:kc + 1]),
                            reads=[b_sc[half], b_const], writes=[b_hT])

        def rope_chunk(p, ntok, out_ap, b_out):
            if DEBUG_KSTEPS < 4:
                return
            B.emit(ACT, lambda e: e.activation(out=qTf[:, 0:ntok], in_=pj[p][:, 0:ntok], func=AF.Copy),
                   reads=[b_pj[p]], writes=[b_qTf])
            if DEBUG_KSTEPS < 5 and ntok == T:
                return
            B.emit(PE, lambda e: e.matmul(rp[:, 0:ntok], permb[:], qTf[:, 0:ntok], start=True, stop=True),
                   reads=[b_qTf, b_const], writes=[b_rp])
            if DEBUG_KSTEPS < 6 and ntok == T:
                return
            B.emit(DVE, lambda e: e.tensor_tensor(out=t1[:, 0:ntok], in0=rp[:, 0:ntok], in1=sinT[:, 0:ntok], op=ALU.mult),
                   reads=[b_rp, b_cs], writes=[b_t1])
            if DEBUG_KSTEPS < 7 and ntok == T:
                return
            B.emit(DVE, lambda e: e.tensor_tensor(out=t2[:, 0:ntok], in0=qTf[:, 0:ntok], in1=cosT[:, 0:ntok], op=ALU.mult),
                   reads=[b_qTf, b_cs], writes=[b_t2])
            if DEBUG_KSTEPS < 8 and ntok == T:
                return
            B.emit(DVE, lambda e: e.tensor_tensor(out=out_ap, in0=t1[:, 0:ntok], in1=t2[:, 0:ntok], op=ALU.add),
                   reads=[b_t1, b_t2], writes=[b_out])

        def phase_a_kv(ntok):
            nblk = ntok // 128
            B.emit(DVE, lambda e: e.tensor_copy(out=kT[:, :, 0:128], in_=kprev[:]), reads=[b_kprev], writes=[b_kT])
            B.emit(DVE, lambda e: e.tensor_copy(out=kT2[:, :, 0:128], in_=k2prev[:]), reads=[b_kprev], writes=[b_kT2])
            B.emit(DVE, lambda e: e.tensor_copy(out=v_sb[:, 0, :], in_=vprev[:]), reads=[b_kprev], writes=[b_v])
            for blk in range(2 if DEBUG_PARTS not in ("v", "none") else 0):
                slot = wload(w_in_d, OFF_K + blk * 256)
                for ch in range(2):
                    c = blk * 2 + ch
                    p = proj_fm(slot, ch, lambda kc: hT[:, kc, 0:ntok], b_hT, ntok)
                    rope_chunk(p, ntok, kT[:, c, 128:128 + ntok], b_kT)
                    if DEBUG_KSTEPS < 9 and ntok == T:
                        continue
                    B.emit(PE, lambda e, c=c: e.matmul(rp[:, 0:ntok], swapb[:], kT[:, c, 128:128 + ntok],
                                                       start=True, stop=True),
                           reads=[b_kT, b_const], writes=[b_rp])
                    B.emit(ACT, lambda e, c=c: e.activation(out=kT2[:, c, 128:128 + ntok], in_=rp[:, 0:ntok], func=AF.Copy),
                           reads=[b_rp], writes=[b_kT2])
            if DEBUG_PARTS in ("k", "none"):
                return
            for half in range(2):
                slot = wload(w_in_d, OFF_V + half * 256)
                for blk in range(nblk):
                    p = proj_tm(slot, lambda kc, blk=blk: hT[:, kc, blk * 128:(blk + 1) * 128], b_hT)
                    B.emit(ACT, lambda e, p=p, blk=blk, half=half: e.activation(
                        out=v_sb[:, 1 + blk, half * 256:(half + 1) * 256], in_=pj[p][:, 0:256], func=AF.Copy),
                        reads=[b_pj[p]], writes=[b_v])

        def save_prev(ntok):
            nblk = ntok // 128
            B.emit(DVE, lambda e: e.tensor_copy(out=kprev[:], in_=kT[:, :, ntok:ntok + 128]), reads=[b_kT], writes=[b_kprev])
            B.emit(DVE, lambda e: e.tensor_copy(out=k2prev[:], in_=kT2[:, :, ntok:ntok + 128]), reads=[b_kT2], writes=[b_kprev])
            B.emit(DVE, lambda e: e.tensor_copy(out=vprev[:], in_=v_sb[:, nblk, :]), reads=[b_v], writes=[b_kprev])

        def attention_group(hk, first_tile):
            for blk in range(2):
                slot = wload(w_in_d, OFF_Q + hk * 512 + blk * 256)
                for ch in range(2):
                    cl = blk * 2 + ch
                    p = proj_fm(slot, ch, lambda kc: hT[:, kc, :], b_hT, T)
                    rope_chunk(p, T, qrot[:, cl, :], b_qrot)
            for blk in range(2):
                slot = wload(w_in_d, OFF_GA + hk * 512 + blk * 256)
                for ch in range(2):
                    cl = blk * 2 + ch
                    p = proj_fm(slot, ch, lambda kc: hT[:, kc, :], b_hT, T)
                    B.emit(ACT, lambda e, p=p, cl=cl: e.activation(out=sgate[:, cl, :], in_=pj[p][:], func=AF.Silu),
                           reads=[b_pj[p]], writes=[b_sgate])
            kc_kv = hk // 2
            for b in range(NBK):
                mk = mask4[1] if (first_tile and b == 0) else mask4[0]
                for half in range(2):
                    def qk(e, half=half, b=b):
                        ins = None
                        for sl in range(4):
                            j = (sl % 2) * 2 + sl // 2
                            g = half * 4 + j
                            cl = g // 2
                            po = (g % 2) * 64
                            ksrc = kT if (g % 2) == (hk % 2) else kT2
                            ins = e.matmul(sc3[:, sl, :], qrot[po:po + 64, cl, b * 128:(b + 1) * 128],
                                           ksrc[po:po + 64, kc_kv, b * 128:b * 128 + 256], start=True, stop=True)
                        return ins
                    if DEBUG_ASTEPS < 1:
                        return
                    B.emit(PE, qk, reads=[b_qrot, b_kT, b_kT2], writes=[b_scall])
                    if DEBUG_ASTEPS < 2:
                        return
                    B.emit(DVE, lambda e, mk=mk: e.tensor_tensor(out=scm, in0=sc3, in1=mk[:], op=ALU.add),
                           reads=[b_scall, b_const], writes=[b_scm])
                    B.emit(DVE, lambda e: e.reduce_max(out=sm_rowmax[:], in_=scm, axis=AX.X),
                           reads=[b_scm], writes=[b_small])
                    if DEBUG_ASTEPS < 3:
                        return
                    h0 = hk * 8 + half * 4
                    B.emit(DVE, lambda e: e.tensor_scalar(out=sm_negm[:], in0=sm_rowmax[:], scalar1=-0.125, scalar2=None,
                                                          op0=ALU.mult), reads=[b_small], writes=[b_small])
                    for par in range(2):
                        B.emit(DVE, lambda e, h0=h0, par=par: e.tensor_tensor(
                            out=sm_negm[:, 2 * par:2 * par + 2], in0=sm_negm[:, 2 * par:2 * par + 2],
                            in1=negsink[:, h0 + par:h0 + par + 3:2], op=ALU.min), reads=[b_small], writes=[b_small])
                    for par in range(2):
                        B.emit(DVE, lambda e, h0=h0, par=par: e.tensor_tensor(
                            out=sm_tmp4[:, 2 * par:2 * par + 2], in0=sm_negm[:, 2 * par:2 * par + 2],
                            in1=sinkB[:, h0 + par:h0 + par + 3:2], op=ALU.add), reads=[b_small], writes=[b_small])
                    if DEBUG_ASTEPS < 4:
                        return
                    for j in range(4):
                        B.emit(ACT, lambda e, j=j: e.activation(out=Pm[:, j, :], in_=scm[:, j, :], func=AF.Exp,
                                                                bias=sm_negm[:, j:j + 1], scale=0.125,
                                                                accum_out=sm_rowsum[:, j:j + 1]),
                               reads=[b_scm, b_small], writes=[b_P, b_small])
                    B.emit(ACT, lambda e: e.activation(out=sm_es[:], in_=sm_tmp4[:], func=AF.Exp),
                           reads=[b_small], writes=[b_small])
                    B.emit(DVE, lambda e: e.tensor_tensor(out=sm_den[:], in0=sm_rowsum[:], in1=sm_es[:], op=ALU.add),
                           reads=[b_small], writes=[b_small])
                    for par in range(2):
                        B.emit(DVE, lambda e, half=half, par=par: e.reciprocal(
                            out=rden[:, half * 4 + par:half * 4 + par + 3:2], in_=sm_den[:, 2 * par:2 * par + 2]),
                            reads=[b_small], writes=[b_rden])

                    if DEBUG_ASTEPS < 5:
                        return

                    def tp(e):
                        ins = None
                        for j in range(4):
                            for s2 in range(2):
                                o = (j * 2 + s2) * 128
                                ins = e.transpose(out=pt[:, o:o + 128], in_=Pm[:, j, s2 * 128:(s2 + 1) * 128],
                                                  identity=identb[:])
                        return ins
                    B.emit(PE, tp, reads=[b_P, b_const], writes=[b_pt])
                    if DEBUG_ASTEPS < 6:
                        return
                    B.emit(ACT, lambda e: e.activation(out=PTs, in_=pt[:], func=AF.Copy), reads=[b_pt], writes=[b_PTs])

                    def pvmm(e, half=half, b=b):
                        ins = None
                        for sl in range(4):
                            g = half * 4 + (sl % 2) * 2 + sl // 2
                            for s2 in range(2):
                                o = (sl * 2 + s2) * 128
                                ins = e.matmul(pv[:, g * 64:(g + 1) * 64], PTs[:, o:o + 128],
                                               v_sb[:, b + s2, hk * 64:(hk + 1) * 64],
                                               start=(s2 == 0), stop=(s2 == 1))
                        return ins
                    B.emit(PE, pvmm, reads=[b_PTs, b_v], writes=[b_pv])
                    if DEBUG_ASTEPS < 7:
                        return
                B.emit(DVE, lambda e: e.tensor_tensor(
                    out=a_tok[:].rearrange("p (g d) -> p g d", g=8),
                    in0=pv[:].rearrange("p (g d) -> p g d", g=8),
                    in1=rden[:].unsqueeze(2).broadcast_to([128, 8, 64]), op=ALU.mult),
                    reads=[b_pv, b_rden], writes=[b_atok])

                if DEBUG_ASTEPS < 8:
                    return

                def trp(e):
                    ins = None
                    for cl in range(4):
                        ins = e.transpose(out=tr[:, cl * 128:(cl + 1) * 128], in_=a_tok[:, cl * 128:(cl + 1) * 128],
                                          identity=identb[:])
                    return ins
                B.emit(PE, trp, reads=[b_atok, b_const], writes=[b_tr])
                B.emit(DVE, lambda e, b=b: e.tensor_tensor(
                    out=br[:, hk * 4:(hk + 1) * 4, b * 128:(b + 1) * 128],
                    in0=tr[:, 0:512].rearrange("p (c t) -> p c t", c=4),
                    in1=sgate[:, :, b * 128:(b + 1) * 128], op=ALU.mult),
                    reads=[b_tr, b_sgate], writes=[b_br])

        def up_attn():
            for i in range(16):
                slot = wload(w_in_d, OFF_MA + i * 256)
                for ch in range(2):
                    p = proj_fm(slot, ch, lambda kc: hT[:, kc, :], b_hT, T)
                    B.emit(ACT, lambda e, p=p, ch=ch: e.activation(out=sig[:, ch, :], in_=pj[p][:], func=AF.Sigmoid),
                           reads=[b_pj[p]], writes=[b_sig])
                slot = wload(w_ua_d, i * 256)
                for ch in range(2):
                    p = proj_fm(slot, ch, lambda kc: br[:, kc, :], b_br, T)
                    B.emit(DVE, lambda e, p=p, ch=ch, i=i: e.tensor_tensor(out=mg[:, 2 * i + ch, :], in0=pj[p][:],
                                                                           in1=sig[:, ch, :], op=ALU.mult),
                           reads=[b_pj[p], b_sig], writes=[b_mg])

        def phase_b():
            for i in range(16):
                slot = wload(w_in_d, OFF_U + i * 256)
                for ch in range(2):
                    p = proj_fm(slot, ch, lambda kc: hT[:, kc, :], b_hT, T)
                    B.emit(ACT, lambda e, p=p, ch=ch: e.activation(out=ub[:, ch, :], in_=pj[p][:], func=AF.Gelu),
                           reads=[b_pj[p]], writes=[b_ub])
                slot = wload(w_in_d, OFF_GB + i * 256)
                for ch in range(2):
                    p = proj_fm(slot, ch, lambda kc: hT[:, kc, :], b_hT, T)
                    B.emit(ACT, lambda e, p=p, ch=ch: e.activation(out=gb[:, ch, :], in_=pj[p][:], func=AF.Silu),
                           reads=[b_pj[p]], writes=[b_gb])
                B.emit(DVE, lambda e, i=i: e.tensor_tensor(out=br[:, 2 * i:2 * i + 2, :], in0=ub[:], in1=gb[:], op=ALU.mult),
                       reads=[b_ub, b_gb], writes=[b_br])
            for i in range(16):
                slot = wload(w_in_d, OFF_VG + i * 256)
                for blk in range(NBK):
                    p = proj_tm(slot, lambda kc, blk=blk: hT[:, kc, blk * 128:(blk + 1) * 128], b_hT)
                    B.emit(ACT, lambda e, p=p, blk=blk, i=i: e.activation(
                        out=gvb[:, blk, i * 256:(i + 1) * 256], in_=pj[p][:, 0:256], func=AF.Gelu),
                        reads=[b_pj[p]], writes=[b_gvb])
            for blk in range(NBK):
                row_stats(gvb[:, blk, :], b_gvb, add_sq=False)
                B.emit(DVE, lambda e, blk=blk: e.tensor_scalar(out=gvb[:, blk, :], in0=gvb[:, blk, :],
                                                               scalar1=mv[:, 0:1], scalar2=rstd[:, 0:1],
                                                               op0=ALU.subtract, op1=ALU.mult),
                       reads=[b_gvb, b_small], writes=[b_gvb])
            n = 0
            for blk in range(NBK):
                for g in range(8):
                    half = n % 2
                    n += 1

                    def spm(e, blk=blk, g=g, half=half):
                        ins = None
                        for cc in range(4):
                            c = g * 4 + cc
                            ins = e.matmul(sc[:, half * 512 + cc * 128: half * 512 + (cc + 1) * 128],
                                           gvb[:, blk, c * 128:(c + 1) * 128], WsT[:, g, :], start=True, stop=True)
                        return ins
                    B.emit(PE, spm, reads=[b_gvb, b_const], writes=[b_sc[half]])
                    for cc in range(4):
                        c = g * 4 + cc
                        B.emit(DVE, lambda e, cc=cc, c=c, g=g: e.scalar_tensor_tensor(
                            out=tmpb[:, cc, :], in0=RSb[:, g, :], scalar=lnbc[:, c:c + 1], in1=BSb[:, g, :],
                            op0=ALU.mult, op1=ALU.add), reads=[b_const], writes=[b_tmpb])
                        B.emit(DVE, lambda e, cc=cc, c=c, half=half: e.scalar_tensor_tensor(
                            out=mixed4[:, cc, :], in0=sc[:, half * 512 + cc * 128: half * 512 + (cc + 1) * 128],
                            scalar=lngc[:, c:c + 1], in1=tmpb[:, cc, :], op0=ALU.mult, op1=ALU.add),
                            reads=[b_sc[half], b_tmpb, b_const], writes=[b_mixed])
                    B.emit(DVE, lambda e, g=g, blk=blk: e.tensor_tensor(
                        out=br[:, g * 4:(g + 1) * 4, blk * 128:(blk + 1) * 128], in0=mixed4[:],
                        in1=br[:, g * 4:(g + 1) * 4, blk * 128:(blk + 1) * 128], op=ALU.mult),
                        reads=[b_mixed, b_br], writes=[b_br])
            for i in range(16):
                slot = wload(w_in_d, OFF_MB + i * 256)
                for ch in range(2):
                    p = proj_fm(slot, ch, lambda kc: hT[:, kc, :], b_hT, T)
                    B.emit(ACT, lambda e, p=p, ch=ch: e.activation(out=sig[:, ch, :], in_=pj[p][:], func=AF.Sigmoid),
                           reads=[b_pj[p]], writes=[b_sig])
                slot = wload(w_ug_d, i * 256)
                for ch in range(2):
                    p = proj_fm(slot, ch, lambda kc: br[:, kc, :], b_br, T)
                    B.emit(DVE, lambda e, p=p, ch=ch: e.tensor_tensor(out=t1[:], in0=pj[p][:], in1=sig[:, ch, :], op=ALU.mult),
                           reads=[b_pj[p], b_sig], writes=[b_t1])
                    B.emit(DVE, lambda e, ch=ch, i=i: e.tensor_tensor(out=mg[:, 2 * i + ch, :], in0=t1[:],
                                                                      in1=mg[:, 2 * i + ch, :], op=ALU.add),
                           reads=[b_t1, b_mg], writes=[b_mg])

        def phase_c(tile):
            r0 = tile * T
            for blk in range(NBK):
                B.dma(SP, [lambda e, blk=blk: e.dma_start(out=ybuf[blk][:, 0:2048],
                                                          in_=x_d[r0 + blk * 128: r0 + (blk + 1) * 128, 0:2048]),
                           lambda e, blk=blk: e.dma_start(out=ybuf[blk][:, 2048:4096],
                                                          in_=x_d[r0 + blk * 128: r0 + (blk + 1) * 128, 2048:4096])],
                      f"y{blk}", writes=[b_y[blk]])
            B.dma(SP, [lambda e: e.dma_start(out=XB[:, 0:2048], in_=bcast_rows(fng_d, 2048)),
                       lambda e: e.dma_start(out=XB[:, 2048:4096], in_=bcast_rows(fng_d, 2048, 2048))],
                  "c", writes=[b_r1b])
            for i in range(16):
                slot = wload(w_out_d, i * 256)
                for blk in range(NBK):
                    p = proj_tm(slot, lambda kc, blk=blk: mg[:, kc, blk * 128:(blk + 1) * 128], b_mg)
                    B.emit(DVE, lambda e, p=p, blk=blk, i=i: e.tensor_tensor(
                        out=ybuf[blk][:, i * 256:(i + 1) * 256], in0=pj[p][:, 0:256],
                        in1=ybuf[blk][:, i * 256:(i + 1) * 256], op=ALU.add),
                        reads=[b_pj[p], b_y[blk]], writes=[b_y[blk]])
            for blk in range(NBK):
                row_stats(ybuf[blk], b_y[blk])
                B.emit(DVE, lambda e, blk=blk: e.scalar_tensor_tensor(
                    out=ybuf[blk][:], in0=ybuf[blk][:], scalar=rstd[:, 0:1], in1=XB[:], op0=ALU.mult, op1=ALU.mult),
                    reads=[b_y[blk], b_small, b_r1b], writes=[b_y[blk]])
                B.dma(SP, [lambda e, blk=blk: e.dma_start(out=y_d[r0 + blk * 128: r0 + (blk + 1) * 128, 0:2048],
                                                          in_=ybuf[blk][:, 0:2048]),
                           lambda e, blk=blk: e.dma_start(out=y_d[r0 + blk * 128: r0 + (blk + 1) * 128, 2048:4096],
                                                          in_=ybuf[blk][:, 2048:4096])],
                      f"y{blk}", reads=[b_y[blk]])

        def program():
            setup()
            B.barrier()
            if stop_after < 1:
                return
            phase_n(xh_d, 1)
            rope_tables(0, 128)
            B.barrier()
            if stop_after < 2:
                return
            B.emit(DVE, lambda e: e.memset(kprev[:], 0.0), writes=[b_kprev])
            B.emit(DVE, lambda e: e.memset(k2prev[:], 0.0), writes=[b_kprev])
            B.emit(DVE, lambda e: e.memset(vprev[:], 0.0), writes=[b_kprev])
            phase_a_kv(128)
            save_prev(128)
            B.barrier()
            if stop_after < 3:
                return
            for tile in range(n_tiles):
                phase_n(x_d[tile * T:(tile + 1) * T, :], NBK)
                rope_tables(128 + tile * T, T)
                B.barrier()
                if stop_after < 4:
                    return
                phase_a_kv(DEBUG_NTOK or T)
                if stop_after < 5:
                    return
                for hk in range(8):
                    attention_group(hk, tile == 0)
                    if stop_after < 6:
                        return
                save_prev(T)
                up_attn()
                B.barrier()
                if stop_after < 7:
                    return
                phase_b()
                B.barrier()
                if stop_after < 8:
                    return
                phase_c(tile)
                if tile < n_tiles - 1:
                    B.barrier(new_group=True)
        program()
        B.final_wait(SP)

        with nc.Block() as block:
            @block.tensor
            def _(e):
                for f in B.q[PE]:
                    f(e)

            @block.scalar
            def _(e):
                for f in B.q[ACT]:
                    f(e)

            @block.vector
            def _(e):
                for f in B.q[DVE]:
                    f(e)

            @block.gpsimd
            def _(e):
                for f in B.q[POOL]:
                    f(e)

            @block.sync
            def _(e):
                for f in B.q[SP]:
                    f(e)
    return nc


def _consts():
    ident = np.eye(128, dtype=np.float32)
    perm = np.zeros((128, 128), np.float32)
    for i in range(128):
        d = i % 64
        if d < 8:
            perm[i + 8, i] = -1.0
        elif d < 16:
            perm[i - 8, i] = 1.0
    swap = np.zeros((128, 128), np.float32)
    for m in range(128):
        swap[(m + 64) % 128, m] = 1.0
    tril = (np.arange(128)[:, None] <= np.arange(128)[None, :]).astype(np.float32)
    invf = np.zeros((128, 1), np.float32)
    half = 8
    inv_freq = (500000.0 ** (-(np.arange(half, dtype=np.float32) * 2.0 / 16.0))).astype(np.float32)
    for p in range(128):
        d = p % 64
        if d < 16:
            invf[p, 0] = inv_freq[d % 8]
    qi = np.arange(128)[:, None]
    si = np.arange(256)[None, :]
    band = (si <= qi + 128) & (si > qi)
    maskA = np.where(band, 0.0, NEG).astype(np.float32)
    maskB0 = np.where(band & (si >= 128), 0.0, NEG).astype(np.float32)
    return dict(ident=ident, perm=perm, swap=swap, tril=tril, invf=invf, maskA=maskA, maskB0=maskB0)


def make_in_map(x_rows, x_halo, pos_ext, first_half, w_in, w_ua, w_ug, w_out, norm_g, lng, lnb, fng, sink,
                w_spatial, b_spatial):
    c = _consts()
    col = lambda v: np.ascontiguousarray(v.reshape(KC, 128).T.astype(np.float32))
    return {
        "x": np.ascontiguousarray(x_rows), "xh": np.ascontiguousarray(x_halo),
        "pos": np.ascontiguousarray(pos_ext.reshape(1, -1).astype(np.int32)),
        "w_in": w_in, "w_ua": w_ua, "w_ug": w_ug, "w_out": w_out,
        "gcol": col(norm_g), "lng": col(lng), "lnb": col(lnb),
        "fng": np.ascontiguousarray(fng.reshape(1, D)), "sink": np.ascontiguousarray(sink.reshape(1, 64)),
        "wspT": np.ascontiguousarray(np.transpose(w_spatial, (2, 0, 1)).reshape(128, 1024)),
        "tril": c["tril"], "bsp": np.ascontiguousarray(b_spatial.reshape(1, 1024)),
        "maskA": c["maskA"], "maskB": c["maskB0"] if first_half else c["maskA"],
        "invf": c["invf"], "ident": c["ident"], "perm": c["perm"], "swap": c["swap"],
    }


_NC_CACHE = {}


def kernel(x, positions, norm_g, w_in, attn_sink, gmlp_ln_g, gmlp_ln_b, w_spatial, b_spatial,
           w_up_attn, w_up_gmlp, w_out, final_norm_g):
    x = np.asarray(x, np.float32)
    positions = np.asarray(positions, np.int32)
    f = lambda a: np.ascontiguousarray(np.asarray(a, np.float32))
    w_in0, w_ua0, w_ug0, w_out0 = f(w_in)[0], f(w_up_attn)[0], f(w_up_gmlp)[0], f(w_out)[0]
    in_maps = []
    for c in range(8):
        b, h = c // 2, c % 2
        s0 = h * TOK_PER_CORE
        rows = x[b, s0:s0 + TOK_PER_CORE]
        if h == 0:
            halo = x[b, 0:128]
            pos_ext = np.concatenate([positions[b, 0:128], positions[b, s0:s0 + TOK_PER_CORE]])
        else:
            halo = x[b, s0 - 128:s0]
            pos_ext = positions[b, s0 - 128:s0 + TOK_PER_CORE]
        in_maps.append(make_in_map(rows, halo, pos_ext, h == 0, w_in0, w_ua0, w_ug0, w_out0,
                                   f(norm_g)[0], f(gmlp_ln_g)[0], f(gmlp_ln_b)[0], f(final_norm_g),
                                   f(attn_sink)[0], f(w_spatial)[0], f(b_spatial)[0]))
    if 4 not in _NC_CACHE:
        _NC_CACHE[4] = build_nc(4)
    nc = _NC_CACHE[4]
    res = run_bass_kernel_spmd(nc, in_maps, core_ids=list(range(8)))
    out = np.empty((4, 4096, D), np.float32)
    for c in range(8):
        b, h = c // 2, c % 2
        out[b, h * TOK_PER_CORE:(h + 1) * TOK_PER_CORE] = res.results[c]["y"]
    return out
```
